# Optimizing a Trainium2 kernel written in Bass

```python
import math
import jax, jax.numpy as jnp
from jax import lax
import numpy as np

D_MODEL = 2048
BATCH = 2
SEQ = 8192
DEPTH = 1

MIX_WIDTH = D_MODEL
CONV_WIDTH = D_MODEL // 2
CONV_K = 3
ATTN_WIDTH = MIX_WIDTH - CONV_WIDTH
DIFF_HEAD_DIM = 64
N_DIFF_HEADS = ATTN_WIDTH // (2 * DIFF_HEAD_DIM)
DIFF_V_DIM = 2 * DIFF_HEAD_DIM
Q_BLOCK = 128
ROPE_THETA = 10000.0
NORM_EPS = 1e-6
SUBLN_EPS = 1e-5
N_KEYS = 128
N_EXPERTS = N_KEYS * N_KEYS
PEER_HEADS = 8
PEER_KEY_DIM = 256
PEER_HALF = PEER_KEY_DIM // 2
PEER_TOPK = 16
PEER_BLOCK = 128
N_ADA = 6
IN_COLS = 3 * CONV_WIDTH + 3 * ATTN_WIDTH

kernel_name = "hymba_shortconv_diffattn_peer_adaln"


def rms_norm(x, g, eps=NORM_EPS):
    xf = x.astype(jnp.float32)
    y = xf * lax.rsqrt(jnp.mean(xf * xf, axis=-1, keepdims=True) + eps)
    return (y * g.astype(jnp.float32)).astype(x.dtype)


def modulate(h, shift, scale):
    return h * (1.0 + scale[:, None, :]) + shift[:, None, :]


def rope_tables(positions, dim):
    inv_freq = ROPE_THETA ** (-jnp.arange(0, dim, 2, dtype=jnp.float32) / dim)
    ang = positions.astype(jnp.float32)[..., None] * inv_freq
    ang = jnp.concatenate([ang, ang], axis=-1)
    return jnp.cos(ang)[:, :, None, None, :], jnp.sin(ang)[:, :, None, None, :]


def apply_rope(x, cos, sin):
    half = x.shape[-1] // 2
    x1, x2 = x[..., :half], x[..., half:]
    rot = jnp.concatenate([-x2, x1], axis=-1)
    return x.astype(jnp.float32) * cos + rot.astype(jnp.float32) * sin


def causal_short_conv(z, w):
    S = z.shape[1]
    zp = jnp.pad(z, ((0, 0), (CONV_K - 1, 0), (0, 0)))
    return sum(w[j] * lax.dynamic_slice_in_dim(zp, j, S, axis=1) for j in range(CONV_K))


def diff_attention(hq, hk, hv, positions, q_norm_g, k_norm_g, lam, subln_g, lam_init):
    B, S, _ = hq.shape
    H, Dh, E = N_DIFF_HEADS, DIFF_HEAD_DIM, DIFF_V_DIM
    q = rms_norm(hq.reshape(B, S, H, 2, Dh), q_norm_g)
    k = rms_norm(hk.reshape(B, S, H, 2, Dh), k_norm_g)
    cos, sin = rope_tables(positions, Dh)
    q32 = apply_rope(q, cos, sin) * (Dh ** -0.5)
    k32 = apply_rope(k, cos, sin)
    v32 = hv.reshape(B, S, H, E).astype(jnp.float32)
    kpos = jnp.arange(S)
    n_blocks = S // Q_BLOCK

    def block(i):
        qb = lax.dynamic_slice_in_dim(q32, i * Q_BLOCK, Q_BLOCK, axis=1)
        s = jnp.einsum('bqhmd,bkhmd->bhmqk', qb, k32)
        qpos = i * Q_BLOCK + jnp.arange(Q_BLOCK)
        mask = kpos[None, :] <= qpos[:, None]
        p = jax.nn.softmax(jnp.where(mask, s, -jnp.inf), axis=-1)
        a = p[:, :, 0] - lam * p[:, :, 1]
        return jnp.einsum('bhqk,bkhe->bqhe', a, v32)

    o = lax.map(block, jnp.arange(n_blocks))
    o = jnp.moveaxis(o, 0, 1).reshape(B, S, H, E)
    o = rms_norm(o, subln_g, SUBLN_EPS) * (1.0 - lam_init)
    return o.reshape(B, S, H * E).astype(hq.dtype)


def peer_ffn(h, w_q, keys1, keys2, expert_u, expert_v):
    B, S, D = h.shape
    T = B * S
    hf = h.reshape(T, D)
    q = (hf @ w_q).reshape(T, PEER_HEADS, 2, PEER_HALF)
    s1 = jnp.einsum('thd,nd->thn', q[:, :, 0], keys1)
    s2 = jnp.einsum('thd,nd->thn', q[:, :, 1], keys2)
    v1, i1 = lax.top_k(s1, PEER_TOPK)
    v2, i2 = lax.top_k(s2, PEER_TOPK)
    cand = (v1[..., :, None] + v2[..., None, :]).reshape(T, PEER_HEADS, PEER_TOPK * PEER_TOPK)
    cand_idx = (i1[..., :, None] * N_KEYS + i2[..., None, :]).reshape(T, PEER_HEADS, PEER_TOPK * PEER_TOPK)
    top_s, top_p = lax.top_k(cand, PEER_TOPK)
    idx = jnp.take_along_axis(cand_idx, top_p, axis=-1)
    g = jax.nn.softmax(top_s.astype(jnp.float32), axis=-1)

    def apply(args):
        xb, ib, gb = args
        u = jnp.take(expert_u, ib, axis=0)
        a = jnp.einsum('thkd,td->thk', u, xb)
        w = (jax.nn.gelu(a.astype(jnp.float32), approximate=False) * gb).astype(xb.dtype)
        vv = jnp.take(expert_v, ib, axis=0)
        return jnp.einsum('thk,thkd->td', w, vv)

    nb = T // PEER_BLOCK
    out = lax.map(apply, (hf.reshape(nb, PEER_BLOCK, D),
                          idx.reshape(nb, PEER_BLOCK, PEER_HEADS, PEER_TOPK),
                          g.reshape(nb, PEER_BLOCK, PEER_HEADS, PEER_TOPK)))
    return out.reshape(B, S, D)


def setup_inputs(seed: int = 0) -> dict:
    key = jax.random.key(seed)
    ks = jax.random.split(key, 24)
    f32 = jnp.float32
    nrm = lambda k, shape, std: jax.random.normal(k, shape, f32) * std
    gain = lambda k, shape: 1.0 + 0.02 * jax.random.normal(k, shape, f32)
    L, D = DEPTH, D_MODEL
    return {
        "x": nrm(ks[0], (BATCH, SEQ, D), 1.0),
        "c": nrm(ks[1], (BATCH, D), 1.0),
        "positions": jnp.broadcast_to(jnp.arange(SEQ, dtype=jnp.int32), (BATCH, SEQ)),
        "w_ada": nrm(ks[2], (L, D, N_ADA * D), D ** -0.5),
        "b_ada": nrm(ks[3], (L, N_ADA * D), 0.02),
        "norm1_g": gain(ks[4], (L, D)),
        "w_in": nrm(ks[5], (L, D, IN_COLS), D ** -0.5),
        "conv_w": nrm(ks[6], (L, CONV_K, CONV_WIDTH), CONV_K ** -0.5),
        "q_norm_g": gain(ks[7], (L, DIFF_HEAD_DIM)),
        "k_norm_g": gain(ks[8], (L, DIFF_HEAD_DIM)),
        "lambda_q1": nrm(ks[9], (L, DIFF_HEAD_DIM), 0.1),
        "lambda_k1": nrm(ks[10], (L, DIFF_HEAD_DIM), 0.1),
        "lambda_q2": nrm(ks[11], (L, DIFF_HEAD_DIM), 0.1),
        "lambda_k2": nrm(ks[12], (L, DIFF_HEAD_DIM), 0.1),
        "subln_g": gain(ks[13], (L, DIFF_V_DIM)),
        "w_out": nrm(ks[14], (L, MIX_WIDTH, D), MIX_WIDTH ** -0.5),
        "norm2_g": gain(ks[15], (L, D)),
        "w_peer_q": nrm(ks[16], (L, D, PEER_HEADS * PEER_KEY_DIM), D ** -0.5),
        "sub_keys1": nrm(ks[17], (L, N_KEYS, PEER_HALF), PEER_HALF ** -0.5),
        "sub_keys2": nrm(ks[18], (L, N_KEYS, PEER_HALF), PEER_HALF ** -0.5),
        "expert_u": nrm(ks[19], (L, N_EXPERTS, D), D ** -0.5),
        "expert_v": nrm(ks[20], (L, N_EXPERTS, D), PEER_HEADS ** -0.5),
    }


def reference(x, c, positions, w_ada, b_ada, norm1_g, w_in, conv_w, q_norm_g, k_norm_g,
              lambda_q1, lambda_k1, lambda_q2, lambda_k2, subln_g, w_out, norm2_g,
              w_peer_q, sub_keys1, sub_keys2, expert_u, expert_v):
    split_at = [CONV_WIDTH, 2 * CONV_WIDTH, 3 * CONV_WIDTH,
                3 * CONV_WIDTH + ATTN_WIDTH, 3 * CONV_WIDTH + 2 * ATTN_WIDTH]
    for l in range(DEPTH):
        lam_init = 0.8 - 0.6 * math.exp(-0.3 * l)
        mod = jax.nn.silu(c) @ w_ada[l] + b_ada[l]
        shift1, scale1, gate1, shift2, scale2, gate2 = jnp.split(mod, N_ADA, axis=-1)

        h = modulate(rms_norm(x, norm1_g[l]), shift1, scale1)
        proj = h @ w_in[l]
        cb, cc, ch, aq, ak, av = jnp.split(proj, split_at, axis=-1)
        y_conv = cb * causal_short_conv(cc * ch, conv_w[l])
        lam = (jnp.exp(jnp.sum(lambda_q1[l].astype(jnp.float32) * lambda_k1[l].astype(jnp.float32)))
               - jnp.exp(jnp.sum(lambda_q2[l].astype(jnp.float32) * lambda_k2[l].astype(jnp.float32)))
               + lam_init)
        y_attn = diff_attention(aq, ak, av, positions, q_norm_g[l], k_norm_g[l],
                                lam, subln_g[l], lam_init)
        mix = jnp.concatenate([y_conv, y_attn], axis=-1) @ w_out[l]
        x = x + gate1[:, None, :] * mix

        h2 = modulate(rms_norm(x, norm2_g[l]), shift2, scale2)
        ffn = peer_ffn(h2, w_peer_q[l], sub_keys1[l], sub_keys2[l], expert_u[l], expert_v[l])
        x = x + gate2[:, None, :] * ffn
    return x
```

```python
import math
import numpy as np
from contextlib import ExitStack
import concourse.bass as bass
import concourse.mybir as mybir
from concourse.bass_utils import run_bass_kernel_spmd

F32 = mybir.dt.float32
BF16 = mybir.dt.bfloat16
I32 = mybir.dt.int32
U32 = mybir.dt.uint32
AF = mybir.ActivationFunctionType
ALU = mybir.AluOpType
AX = mybir.AxisListType

D = 2048
DC = 16
NEG = -30000.0


class Res:
    __slots__ = ("name", "w", "r")

    def __init__(self, name=""):
        self.name = name
        self.w = None
        self.r = []


class K:
    def __init__(self, nc, n_dma_slots=8, n_gp_slots=4):
        self.nc = nc
        self.es = ExitStack()
        self.scopes = []
        self.eng = {"pe": nc.tensor, "dve": nc.vector, "act": nc.scalar, "pool": nc.gpsimd, "sp": nc.sync}
        self.sem = {}
        self.cnt = {}
        for e in ("pe", "dve", "act", "pool"):
            self.sem[e] = self.es.enter_context(nc.semaphore("s_" + e))
            self.cnt[e] = 0
        self.waited = {}
        self.keep = []
        self.last = {}
        self.slots = {}
        for q, n in (("sp", n_dma_slots), ("pool", n_gp_slots)):
            self.slots[q] = [[self.es.enter_context(nc.semaphore("d_%s%d" % (q, i))), 0] for i in range(n)]
        self.slot_i = {"sp": 0, "pool": 0}
        self.n_inst = 0

    def push(self):
        self.scopes.append(ExitStack())

    def pop(self):
        self.barrier()
        self.scopes.pop().close()

    def _stack(self):
        return self.scopes[-1] if self.scopes else self.es

    def sb(self, name, shape, dt):
        return self._stack().enter_context(self.nc.sbuf_tensor("sb_" + name, list(shape), dt))

    def ps(self, name, shape, dt=F32):
        return self._stack().enter_context(self.nc.psum_tensor("ps_" + name, list(shape), dt))

    def _wait(self, e, tok):
        sem, val, src = tok
        if src == e and e == "pe":
            return
        key = (e, id(sem))
        if self.waited.get(key, 0) >= val:
            return
        self.eng[e].wait_ge(sem, val)
        self.waited[key] = val
        self.n_inst += 1

    def _deps(self, e, reads, writes):
        for r in reads:
            if r.w is not None:
                self._wait(e, r.w)
        for w in writes:
            if w.w is not None:
                self._wait(e, w.w)
            for t in w.r:
                self._wait(e, t)

    def _commit(self, tok, reads, writes):
        for r in reads:
            r.r.append(tok)
            if len(r.r) > 48:
                d = {}
                for t in r.r:
                    d[id(t[0])] = t if (id(t[0]) not in d or d[id(t[0])][1] < t[1]) else d[id(t[0])]
                r.r = list(d.values())
        for w in writes:
            w.w = tok
            w.r = []

    SEM_LIMIT = 5000

    def op(self, e, fn, reads=(), writes=()):
        self._deps(e, reads, writes)
        inst = fn(self.eng[e])
        if self.cnt[e] >= self.SEM_LIMIT:
            self.keep.append(self.sem[e])
            self.sem[e] = self.es.enter_context(self.nc.semaphore("s_%s_%d" % (e, self.n_inst)))
            self.cnt[e] = 0
        self.cnt[e] += 1
        inst.then_inc(self.sem[e], 1)
        tok = (self.sem[e], self.cnt[e], e)
        self.last[e] = tok
        self._commit(tok, reads, writes)
        self.n_inst += 1
        return inst

    def dma(self, q, fn, reads=(), writes=()):
        self._deps(q, reads, writes)
        sl = self.slots[q]
        i = self.slot_i[q]
        self.slot_i[q] = (i + 1) % len(sl)
        sem, v = sl[i]
        if v > 0:
            self._wait(q, (sem, v, "dma"))
        if v >= self.SEM_LIMIT:
            self.keep.append(sem)
            sem = self.es.enter_context(self.nc.semaphore("d_%s%d_%d" % (q, i, self.n_inst)))
            sl[i][0] = sem
            v = 0
        inst = fn(self.eng[q])
        v += 16
        sl[i][1] = v
        inst.then_inc(sem, 16)
        tok = (sem, v, "dma")
        self._commit(tok, reads, writes)
        self.n_inst += 1
        return inst

    def barrier(self):
        toks = [self.last[e] for e in ("pe", "dve", "act", "pool") if e in self.last]
        for q in self.slots:
            for sem, v in self.slots[q]:
                if v > 0:
                    toks.append((sem, v, "dma"))
        for e in ("pe", "dve", "act", "pool", "sp"):
            for t in toks:
                if t[2] == e:
                    continue
                self._wait(e, t)


def build(S, debug=False):
    T = S
    NTB = T // 128
    TO = T // 4
    NTO = TO // 128
    NQC = TO // 512
    EXT = 128 + TO
    assert TO % 512 == 0
    nc = bass.Bass("TRN2", target_bir_lowering=False)

    def din(name, shape, dt=F32):
        return nc.dram_tensor(name, list(shape), dt, kind="ExternalInput").ap()

    def dscr(name, shape, dt=F32):
        return nc.dram_tensor(name, list(shape), dt, kind=("ExternalOutput" if debug else "Internal")).ap()

    xb_d = din("xb", [T, D])
    xo_d = din("xo", [EXT, D])
    c_d = din("c", [1, D])
    posb_d = din("posb", [NTB, 128], I32)
    poso_d = din("poso", [NTO, 128], I32)
    halo_d = din("halo", [128, 1])
    bias0_d = din("bias0", [128, NTB * NQC])
    msel_d = din("msel", [16, 128, 512])
    w_ada_d = din("w_ada", [D, 6 * D])
    b_ada_d = din("b_ada", [1, 6 * D])
    n1g_d = din("norm1_g", [1, D])
    w_in_d = din("w_in", [D, 6144])
    conv_w_d = din("conv_w", [3, 1024])
    qng_d = din("q_norm_g", [1, 64])
    kng_d = din("k_norm_g", [1, 64])
    lq1_d = din("lambda_q1", [1, 64])
    lk1_d = din("lambda_k1", [1, 64])
    lq2_d = din("lambda_q2", [1, 64])
    lk2_d = din("lambda_k2", [1, 64])
    sublng_d = din("subln_g", [1, 128])
    w_out_d = din("w_out", [D, D])
    n2g_d = din("norm2_g", [1, D])
    wpq_d = din("w_peer_q", [D, D])
    sk1_d = din("sub_keys1", [128, 128])
    sk2_d = din("sub_keys2", [128, 128])
    eu_d = din("expert_u", [16384, D])
    ev_d = din("expert_v", [16384, D])
    out_d = nc.dram_tensor("out", [TO, D], F32, kind="ExternalOutput").ap()

    KT_d = dscr("KT_s", [128, 8, T], BF16)
    V_d = dscr("V_s", [T, 1024], BF16)
    QT_d = dscr("QT_s", [128, 8, TO], BF16)
    Y_d = dscr("Y_s", [128, 16, TO], BF16)
    MOD_d = dscr("MOD_s", [4, D], F32)
    X1_d = dscr("X1_s", [TO, D], F32)
    H2_d = dscr("H2_s", [TO, D], F32)
    r_KT, r_V, r_QT, r_Y, r_MOD, r_X1, r_H2, r_out = (Res(n) for n in ("KT", "V", "QT", "Y", "MOD", "X1", "H2", "out"))

    k = K(nc)
    R = Res

    ident = k.sb("ident", [128, 128], BF16); r_ident = R()
    k.op("pool", lambda e: e.memset(ident[:], 1.0), writes=[r_ident])
    k.op("pool", lambda e: e.affine_select(out=ident[:], in_=ident[:], pattern=[[-1, 128]], compare_op=ALU.is_equal,
                                           fill=0.0, base=0, channel_multiplier=1), reads=[r_ident], writes=[r_ident])
    kg = k.sb("kg", [128, 64], F32); r_kg = R()
    qg = k.sb("qg", [128, 64], F32); r_qg = R()
    halo = k.sb("halo", [128, 1], F32); r_halo = R()
    k.push()
    G1 = k.sb("G1", [128, D], F32); r_G1 = R()
    SH1 = k.sb("SH1", [128, D], F32); r_SH1 = R()
    scb = k.sb("scb_tab", [128, NTB, 2, 32], F32); r_scb = R()
    sco = k.sb("sco_tab", [128, NTO, 2, 32], F32); r_sco = R()
    k.dma("sp", lambda e: e.dma_start(out=kg[:], in_=kng_d[0:1, :].to_broadcast([128, 64])), writes=[r_kg])
    k.dma("sp", lambda e: e.dma_start(out=qg[:], in_=qng_d[0:1, :].to_broadcast([128, 64])), writes=[r_qg])
    k.dma("sp", lambda e: e.dma_start(out=halo[:], in_=halo_d[:, :]), writes=[r_halo])

    def rope_table(tab, r_tab, pos_d, NT, nm):
        k.push()
        pi = k.sb(nm + "pi", [128, NT], I32); r_pi = R()
        pf = k.sb(nm + "pf", [128, NT], F32); r_pf = R()
        invf = k.sb(nm + "invf", [128, 32], F32); r_invf = R()
        ang = k.sb(nm + "ang", [128, NT, 2, 32], F32); r_ang = R()
        u = k.sb(nm + "u", [128, NT, 2, 32], F32); r_u = R()
        ni = k.sb(nm + "ni", [128, NT, 2, 32], I32); r_ni = R()
        for n0 in range(0, NT, 8):
            n1 = min(NT, n0 + 8)
            k.dma("sp", lambda e: e.dma_start(out=pi[:, n0:n1], in_=pos_d[n0:n1, :].rearrange("n p -> p n"), allow_slow_non_contiguous=True), writes=[r_pi])
        k.op("dve", lambda e: e.tensor_copy(out=pf[:], in_=pi[:]), reads=[r_pi], writes=[r_pf])
        for i in range(32):
            val = float(np.float32(10000.0) ** np.float32(-(2.0 * i) / 64.0))
            k.op("pool", lambda e: e.memset(invf[:, i:i + 1], val), writes=[r_invf])
        for j in range(2):
            k.op("dve", lambda e: e.tensor_tensor(out=ang[:, :, j, :], in0=pf[:].unsqueeze(2).to_broadcast([128, NT, 32]),
                                                  in1=invf[:].unsqueeze(1).to_broadcast([128, NT, 32]), op=ALU.mult),
                 reads=[r_pf, r_invf], writes=[r_ang])
        k.op("dve", lambda e: e.tensor_scalar(out=ang[:, :, 1, :], in0=ang[:, :, 1, :], scalar1=float(math.pi / 2), scalar2=None, op0=ALU.add),
             reads=[r_ang], writes=[r_ang])
        C1 = 6.28125
        C2 = float(2 * math.pi - 6.28125)
        k.op("dve", lambda e: e.tensor_scalar(out=u[:], in0=ang[:], scalar1=float(1.0 / (2 * math.pi)), scalar2=None, op0=ALU.mult), reads=[r_ang], writes=[r_u])
        k.op("dve", lambda e: e.tensor_copy(out=ni[:], in_=u[:]), reads=[r_u], writes=[r_ni])
        k.op("dve", lambda e: e.tensor_copy(out=u[:], in_=ni[:]), reads=[r_ni], writes=[r_u])
        k.op("dve", lambda e: e.scalar_tensor_tensor(out=ang[:], in0=u[:], scalar=-C1, in1=ang[:], op0=ALU.mult, op1=ALU.add), reads=[r_u, r_ang], writes=[r_ang])
        k.op("dve", lambda e: e.scalar_tensor_tensor(out=ang[:], in0=u[:], scalar=-C2, in1=ang[:], op0=ALU.mult, op1=ALU.add), reads=[r_u, r_ang], writes=[r_ang])
        k.op("dve", lambda e: e.tensor_scalar(out=u[:], in0=ang[:], scalar1=float(math.pi), scalar2=float(-2 * math.pi), op0=ALU.is_gt, op1=ALU.mult), reads=[r_ang], writes=[r_u])
        k.op("dve", lambda e: e.tensor_tensor(out=ang[:], in0=ang[:], in1=u[:], op=ALU.add), reads=[r_ang, r_u], writes=[r_ang])
        k.op("dve", lambda e: e.tensor_scalar(out=u[:], in0=ang[:], scalar1=float(-math.pi), scalar2=float(2 * math.pi), op0=ALU.is_lt, op1=ALU.mult), reads=[r_ang], writes=[r_u])
        k.op("dve", lambda e: e.tensor_tensor(out=ang[:], in0=ang[:], in1=u[:], op=ALU.add), reads=[r_ang, r_u], writes=[r_ang])
        k.op("dve", lambda e: e.tensor_scalar(out=ang[:], in0=ang[:], scalar1=3.1415925, scalar2=-3.1415925, op0=ALU.min, op1=ALU.max), reads=[r_ang], writes=[r_ang])
        k.op("act", lambda e: e.activation(out=tab[:], in_=ang[:], func=AF.Sin), reads=[r_ang], writes=[r_tab])
        k.pop()

    rope_table(scb, r_scb, posb_d, NTB, "rb")
    rope_table(sco, r_sco, poso_d, NTO, "ro")

    k.push()
    cT = k.sb("cT", [128, DC], F32); r_cT = R()
    sT = k.sb("sT", [128, DC], F32); r_sT = R()
    sbc = k.sb("sbc", [128, DC, 128], F32); r_sbc = R()
    wst = [k.sb("ada_w%d" % i, [128, D], F32) for i in range(2)]; r_wst = [R(), R()]
    bb = k.sb("ada_b", [128, D], F32); r_bb = R()
    gg = k.sb("ada_g", [128, D], F32); r_gg = R()
    mt = k.sb("ada_m", [128, D], F32); r_mt = R()
    pA = k.ps("pA0", [128, D], F32); r_pA = R()
    k.dma("sp", lambda e: e.dma_start(out=cT[:], in_=c_d.rearrange("o (c p) -> p (o c)", p=128), allow_slow_non_contiguous=True), writes=[r_cT])
    k.op("act", lambda e: e.activation(out=sT[:], in_=cT[:], func=AF.Silu), reads=[r_cT], writes=[r_sT])
    k.op("dve", lambda e: e.tensor_copy(out=sbc[:], in_=sT[:].unsqueeze(2).to_broadcast([128, DC, 128])), reads=[r_sT], writes=[r_sbc])
    wi = 0
    for j in range(6):
        for dc in range(DC):
            w = wst[wi % 2]; rw = r_wst[wi % 2]; wi += 1
            k.dma("sp", lambda e: e.dma_start(out=w[:], in_=w_ada_d[dc * 128:(dc + 1) * 128, j * D:(j + 1) * D]), writes=[rw])
            for nb in range(4):
                k.op("pe", lambda e: e.matmul(pA[:, nb * 512:(nb + 1) * 512], lhsT=sbc[:, dc, :], rhs=w[:, nb * 512:(nb + 1) * 512],
                                              start=(dc == 0), stop=(dc == DC - 1)), reads=[r_sbc, rw], writes=[r_pA])
        k.dma("sp", lambda e: e.dma_start(out=bb[:], in_=b_ada_d[0:1, j * D:(j + 1) * D].to_broadcast([128, D])), writes=[r_bb])
        if j == 0:
            k.op("dve", lambda e: e.tensor_tensor(out=SH1[:], in0=pA[:], in1=bb[:], op=ALU.add), reads=[r_pA, r_bb], writes=[r_SH1])
        elif j in (1, 4):
            gsrc = n1g_d if j == 1 else n2g_d
            k.dma("sp", lambda e: e.dma_start(out=gg[:], in_=gsrc[0:1, :].to_broadcast([128, D])), writes=[r_gg])
            dst, rdst = (G1, r_G1) if j == 1 else (mt, r_mt)
            k.op("dve", lambda e: e.tensor_tensor(out=dst[:], in0=pA[:], in1=bb[:], op=ALU.add), reads=[r_pA, r_bb], writes=[rdst])
            k.op("dve", lambda e: e.scalar_tensor_tensor(out=dst[:], in0=dst[:], scalar=1.0, in1=gg[:], op0=ALU.add, op1=ALU.mult), reads=[rdst, r_gg], writes=[rdst])
            if j == 4:
                k.dma("sp", lambda e: e.dma_start(out=MOD_d[1:2, :], in_=mt[0:1, :]), reads=[r_mt], writes=[r_MOD])
        else:
            k.op("dve", lambda e: e.tensor_tensor(out=mt[:], in0=pA[:], in1=bb[:], op=ALU.add), reads=[r_pA, r_bb], writes=[r_mt])
            row = {2: 0, 3: 2, 5: 3}[j]
            k.dma("sp", lambda e: e.dma_start(out=MOD_d[row:row + 1, :], in_=mt[0:1, :]), reads=[r_mt], writes=[r_MOD])
    k.pop()

    def load_weight_bf16(dst, r_dst, src_d, col0, ncols, wstg, r_wstg, cast_engs=("dve", "pool")):
        i = 0
        for dc in range(DC):
            for c0 in range(0, ncols, 2048):
                cw = min(2048, ncols - c0)
                st = wstg[i % 2]; rs = r_wstg[i % 2]
                k.dma("sp", lambda e: e.dma_start(out=st[:, 0:cw], in_=src_d[dc * 128:(dc + 1) * 128, col0 + c0:col0 + c0 + cw]), writes=[rs])
                eng = cast_engs[i % len(cast_engs)]
                k.op(eng, lambda e: e.tensor_copy(out=dst[:, dc, c0:c0 + cw], in_=st[:, 0:cw]), reads=[rs], writes=[r_dst])
                i += 1

    def norm_mod(xt, r_xt, Gt, r_Gt, St, r_St, junk, r_junk, ss, r_ss, tmp, r_tmp, hb, r_hb, hf=None, r_hf=None):
        k.op("act", lambda e: e.activation(out=junk[:], in_=xt[:], func=AF.Square, accum_out=ss[:, 0:1]), reads=[r_xt], writes=[r_junk, r_ss])
        k.op("dve", lambda e: e.tensor_scalar(out=ss[:, 1:2], in0=ss[:, 0:1], scalar1=1.0 / D, scalar2=1e-6, op0=ALU.mult, op1=ALU.add), reads=[r_ss], writes=[r_ss])
        k.op("act", lambda e: e.activation(out=ss[:, 2:3], in_=ss[:, 1:2], func=AF.Sqrt), reads=[r_ss], writes=[r_ss])
        k.op("dve", lambda e: e.reciprocal(out=ss[:, 3:4], in_=ss[:, 2:3]), reads=[r_ss], writes=[r_ss])
        k.op("dve", lambda e: e.scalar_tensor_tensor(out=tmp[:], in0=xt[:], scalar=ss[:, 3:4], in1=Gt[:], op0=ALU.mult, op1=ALU.mult),
             reads=[r_xt, r_ss, r_Gt], writes=[r_tmp])
        if hf is not None:
            k.op("dve", lambda e: e.tensor_tensor(out=hf[:], in0=tmp[:], in1=St[:], op=ALU.add), reads=[r_tmp, r_St], writes=[r_hf])
            k.op("pool", lambda e: e.tensor_copy(out=hb[:], in_=hf[:]), reads=[r_hf], writes=[r_hb])
        else:
            k.op("dve", lambda e: e.tensor_tensor(out=hb[:], in0=tmp[:], in1=St[:], op=ALU.add), reads=[r_tmp, r_St], writes=[r_hb])

    def transpose_cols(src_bf, r_src, nch, pT, r_pT, dst, r_dst, evac="act"):
        for c in range(nch):
            k.op("pe", lambda e: e.transpose(out=pT[:, c, :], in_=src_bf[:, c * 128:(c + 1) * 128], identity=ident[:]),
                 reads=[r_src, r_ident], writes=[r_pT])
        if evac == "act":
            k.op("act", lambda e: e.activation(out=dst, in_=pT[:, 0:nch, :], func=AF.Copy), reads=[r_pT], writes=[r_dst])
        else:
            k.op("dve", lambda e: e.tensor_copy(out=dst, in_=pT[:, 0:nch, :]), reads=[r_pT], writes=[r_dst])

    def qk_post(src_ps, r_src, gt, r_gt, sc_tab, r_tab, ti, W, out_bf, r_out):
        sq, ssk, kn, t1, t2 = W["sq"], W["ssk"], W["kn"], W["t1"], W["t2"]
        r_sq, r_ssk, r_kn, r_t1, r_t2 = W["r_sq"], W["r_ssk"], W["r_kn"], W["r_t1"], W["r_t2"]
        k.op("act", lambda e: e.activation(out=sq[:].rearrange("p g d -> p (g d)"), in_=src_ps, func=AF.Square), reads=[r_src], writes=[r_sq])
        k.op("dve", lambda e: e.tensor_reduce(out=ssk[:, 0, :], in_=sq[:], axis=AX.X, op=ALU.add), reads=[r_sq], writes=[r_ssk])
        k.op("dve", lambda e: e.tensor_scalar(out=ssk[:, 1, :], in0=ssk[:, 0, :], scalar1=1.0 / 64, scalar2=1e-6, op0=ALU.mult, op1=ALU.add), reads=[r_ssk], writes=[r_ssk])
        k.op("act", lambda e: e.activation(out=ssk[:, 2, :], in_=ssk[:, 1, :], func=AF.Sqrt), reads=[r_ssk], writes=[r_ssk])
        k.op("dve", lambda e: e.reciprocal(out=ssk[:, 3, :], in_=ssk[:, 2, :]), reads=[r_ssk], writes=[r_ssk])
        k.op("dve", lambda e: e.tensor_tensor(out=kn[:], in0=src_ps.rearrange("p (g d) -> p g d", d=64),
                                              in1=ssk[:, 3, :].unsqueeze(2).to_broadcast([128, 16, 64]), op=ALU.mult), reads=[r_src, r_ssk], writes=[r_kn])
        k.op("dve", lambda e: e.tensor_tensor(out=kn[:], in0=kn[:], in1=gt[:].unsqueeze(1).to_broadcast([128, 16, 64]), op=ALU.mult), reads=[r_kn, r_gt], writes=[r_kn])
        sinb = sc_tab[:, ti, 0, :].unsqueeze(1).to_broadcast([128, 16, 32])
        cosb = sc_tab[:, ti, 1, :].unsqueeze(1).to_broadcast([128, 16, 32])
        x1 = kn[:, :, 0:32]
        x2 = kn[:, :, 32:64]
        k.op("dve", lambda e: e.tensor_tensor(out=t1[:], in0=x1, in1=cosb, op=ALU.mult), reads=[r_kn, r_tab], writes=[r_t1])
        k.op("pool", lambda e: e.tensor_tensor(out=t2[:], in0=x2, in1=sinb, op=ALU.mult), reads=[r_kn, r_tab], writes=[r_t2])
        k.op("dve", lambda e: e.tensor_tensor(out=out_bf[:, :, 0:32], in0=t1[:], in1=t2[:], op=ALU.subtract), reads=[r_t1, r_t2], writes=[r_out])
        k.op("dve", lambda e: e.tensor_tensor(out=t1[:], in0=x2, in1=cosb, op=ALU.mult), reads=[r_kn, r_tab], writes=[r_t1])
        k.op("pool", lambda e: e.tensor_tensor(out=t2[:], in0=x1, in1=sinb, op=ALU.mult), reads=[r_kn, r_tab], writes=[r_t2])
        k.op("dve", lambda e: e.tensor_tensor(out=out_bf[:, :, 32:64], in0=t1[:], in1=t2[:], op=ALU.add), reads=[r_t1, r_t2], writes=[r_out])

    def qk_work(nm):
        W = {}
        for n, shp, dt in (("sq", [128, 16, 64], F32), ("ssk", [128, 4, 16], F32), ("kn", [128, 16, 64], F32),
                           ("t1", [128, 16, 32], F32), ("t2", [128, 16, 32], F32)):
            W[n] = k.sb(nm + n, shp, dt)
            W["r_" + n] = R()
        return W

    k.push()
    Wkv = k.sb("Wkv", [128, DC, 2048], BF16); r_Wkv = R()
    wstg = [k.sb("wstg%d" % i, [128, 2048], F32) for i in range(2)]; r_wstg = [R(), R()]
    load_weight_bf16(Wkv, r_Wkv, w_in_d, 4096, 2048, wstg, r_wstg)
    xt = [k.sb("xt%d" % i, [128, D], F32) for i in range(2)]; r_xt = [R(), R()]
    junk = k.sb("junk", [128, D], BF16); r_junk = R()
    ss = k.sb("ss", [128, 4], F32); r_ss = R()
    tmp = k.sb("tmp", [128, D], F32); r_tmp = R()
    hb = k.sb("hb", [128, D], BF16); r_hb = R()
    hT = k.sb("hT", [128, DC, 128], BF16); r_hT = R()
    Wk = qk_work("k_")
    kr = k.sb("kr", [128, 16, 64], BF16); r_kr = R()
    kTt = k.sb("kTt", [128, 8, 128], BF16); r_kTt = R()
    vt = k.sb("vt", [128, 1024], BF16); r_vt = R()
    pA = k.ps("pA1", [128, 2048], F32); r_pA = R()
    pT = k.ps("pT1", [128, DC, 128], BF16); r_pT = R()
    for tt in range(NTB):
        x_ = xt[tt % 2]; rx = r_xt[tt % 2]
        k.dma("sp", lambda e: e.dma_start(out=x_[:], in_=xb_d[tt * 128:(tt + 1) * 128, :]), writes=[rx])
        norm_mod(x_, rx, G1, r_G1, SH1, r_SH1, junk, r_junk, ss, r_ss, tmp, r_tmp, hb, r_hb)
        transpose_cols(hb, r_hb, DC, pT, r_pT, hT[:], r_hT)
        for nb in range(4):
            for dc in range(DC):
                k.op("pe", lambda e: e.matmul(pA[:, nb * 512:(nb + 1) * 512], lhsT=hT[:, dc, :], rhs=Wkv[:, dc, nb * 512:(nb + 1) * 512],
                                              start=(dc == 0), stop=(dc == DC - 1)), reads=[r_hT, r_Wkv], writes=[r_pA])
        k.op("act", lambda e: e.activation(out=vt[:], in_=pA[:, 1024:2048], func=AF.Copy), reads=[r_pA], writes=[r_vt])
        k.dma("sp", lambda e: e.dma_start(out=V_d[tt * 128:(tt + 1) * 128, :], in_=vt[:]), reads=[r_vt], writes=[r_V])
        qk_post(pA[:, 0:1024], r_pA, kg, r_kg, scb, r_scb, tt, Wk, kr, r_kr)
        transpose_cols(kr[:].rearrange("p g d -> p (g d)"), r_kr, 8, pT, r_pT, kTt[:], r_kTt, evac="dve")
        k.dma("sp", lambda e: e.dma_start(out=KT_d[:, :, tt * 128:(tt + 1) * 128], in_=kTt[:]), reads=[r_kTt], writes=[r_KT])
    k.pop()

    k.push()
    hTe = k.sb("hTe", [128, DC, EXT], BF16); r_hTe = R()
    pA = k.ps("pA2", [128, 2048], F32); r_pA = R()
    pT = k.ps("pT2", [128, DC, 128], BF16); r_pT = R()
    k.push()
    xt = [k.sb("xto%d" % i, [128, D], F32) for i in range(2)]; r_xt = [R(), R()]
    junk = k.sb("junk2", [128, D], BF16); r_junk = R()
    ss = k.sb("ss2", [128, 4], F32); r_ss = R()
    tmp = k.sb("tmp2", [128, D], F32); r_tmp = R()
    hb = k.sb("hb2", [128, D], BF16); r_hb = R()
    for ti in range(NTO + 1):
        x_ = xt[ti % 2]; rx = r_xt[ti % 2]
        k.dma("sp", lambda e: e.dma_start(out=x_[:], in_=xo_d[ti * 128:(ti + 1) * 128, :]), writes=[rx])
        norm_mod(x_, rx, G1, r_G1, SH1, r_SH1, junk, r_junk, ss, r_ss, tmp, r_tmp, hb, r_hb)
        transpose_cols(hb, r_hb, DC, pT, r_pT, hTe[:, :, ti * 128:(ti + 1) * 128], r_hTe)
    k.pop()
    k.push()
    Wq = k.sb("Wq", [128, DC, 1024], BF16); r_Wq = R()
    wstg = [k.sb("wstgq%d" % i, [128, 2048], F32) for i in range(2)]; r_wstg = [R(), R()]
    load_weight_bf16(Wq, r_Wq, w_in_d, 3072, 1024, wstg, r_wstg)
    Wk = qk_work("q_")
    qr = k.sb("qr", [128, 16, 64], BF16); r_qr = R()
    qTt = k.sb("qTt", [128, 8, 128], BF16); r_qTt = R()
    for ti in range(NTO):
        c0 = 128 + ti * 128
        for nb in range(2):
            for dc in range(DC):
                k.op("pe", lambda e: e.matmul(pA[:, nb * 512:(nb + 1) * 512], lhsT=hTe[:, dc, c0:c0 + 128], rhs=Wq[:, dc, nb * 512:(nb + 1) * 512],
                                              start=(dc == 0), stop=(dc == DC - 1)), reads=[r_hTe, r_Wq], writes=[r_pA])
        qk_post(pA[:, 0:1024], r_pA, qg, r_qg, sco, r_sco, ti, Wk, qr, r_qr)
        transpose_cols(qr[:].rearrange("p g d -> p (g d)"), r_qr, 8, pT, r_pT, qTt[:], r_qTt, evac="dve")
        k.dma("sp", lambda e: e.dma_start(out=QT_d[:, :, ti * 128:(ti + 1) * 128], in_=qTt[:]), reads=[r_qTt], writes=[r_QT])
    k.pop()
    k.push()
    cw = k.sb("cw", [128, 3, 8], F32); r_cw = R()
    k.dma("sp", lambda e: e.dma_start(out=cw[:], in_=conv_w_d.rearrange("j (c p) -> p j c", p=128), allow_slow_non_contiguous=True), writes=[r_cw])
    wcs = [k.sb("wcs%d" % i, [128, 3, 128], F32) for i in range(2)]; r_wcs = [R(), R()]
    Wc = k.sb("Wc", [128, DC, 3, 128], BF16); r_Wc = R()
    Bs = k.sb("Bs", [128, TO], F32); r_Bs = R()
    Hs = k.sb("Hs", [128, 512], F32); r_Hs = R()
    ze = k.sb("ze", [128, EXT], F32); r_ze = R()
    acc = k.sb("cacc", [128, TO], F32); r_acc = R()
    ybf = k.sb("ybf", [128, TO], BF16); r_ybf = R()
    for cc in range(8):
        for dc in range(DC):
            st = wcs[dc % 2]; rs = r_wcs[dc % 2]
            k.dma("sp", lambda e: e.dma_start(out=st[:], in_=w_in_d[dc * 128:(dc + 1) * 128, :].rearrange("p (j c) -> p j c", c=1024)[:, 0:3, cc * 128:(cc + 1) * 128]),
                  writes=[rs])
            k.op("dve", lambda e: e.tensor_copy(out=Wc[:, dc, :, :], in_=st[:]), reads=[rs], writes=[r_Wc])
        chunks = [(0, 128)] + [(128 + i * 512, 512) for i in range(TO // 512)]
        for (e0, n) in chunks:
            for j in range(3):
                if j == 0 and e0 == 0:
                    continue
                for dc in range(DC):
                    k.op("pe", lambda e: e.matmul(pA[:, j * 512:j * 512 + n], lhsT=Wc[:, dc, j, :], rhs=hTe[:, dc, e0:e0 + n],
                                                  start=(dc == 0), stop=(dc == DC - 1)), reads=[r_Wc, r_hTe], writes=[r_pA])
            if e0 > 0:
                k.op("act", lambda e: e.activation(out=Bs[:, e0 - 128:e0 - 128 + n], in_=pA[:, 0:n], func=AF.Copy), reads=[r_pA], writes=[r_Bs])
            k.op("act", lambda e: e.activation(out=Hs[:, 0:n], in_=pA[:, 1024:1024 + n], func=AF.Copy), reads=[r_pA], writes=[r_Hs])
            k.op("dve", lambda e: e.tensor_tensor(out=ze[:, e0:e0 + n], in0=pA[:, 512:512 + n], in1=Hs[:, 0:n], op=ALU.mult), reads=[r_pA, r_Hs], writes=[r_ze])
            if e0 == 0:
                k.op("dve", lambda e: e.tensor_scalar(out=ze[:, 0:128], in0=ze[:, 0:128], scalar1=halo[:, 0:1], scalar2=None, op0=ALU.mult), reads=[r_ze, r_halo], writes=[r_ze])
        k.op("dve", lambda e: e.tensor_scalar(out=acc[:], in0=ze[:, 128:EXT], scalar1=cw[:, 2, cc:cc + 1], scalar2=None, op0=ALU.mult), reads=[r_ze, r_cw], writes=[r_acc])
        k.op("dve", lambda e: e.scalar_tensor_tensor(out=acc[:], in0=ze[:, 127:EXT - 1], scalar=cw[:, 1, cc:cc + 1], in1=acc[:], op0=ALU.mult, op1=ALU.add), reads=[r_ze, r_cw, r_acc], writes=[r_acc])
        k.op("dve", lambda e: e.scalar_tensor_tensor(out=acc[:], in0=ze[:, 126:EXT - 2], scalar=cw[:, 0, cc:cc + 1], in1=acc[:], op0=ALU.mult, op1=ALU.add), reads=[r_ze, r_cw, r_acc], writes=[r_acc])
        k.op("dve", lambda e: e.tensor_tensor(out=ybf[:], in0=acc[:], in1=Bs[:], op=ALU.mult), reads=[r_acc, r_Bs], writes=[r_ybf])
        k.dma("sp", lambda e: e.dma_start(out=Y_d[:, cc, :], in_=ybf[:]), reads=[r_ybf], writes=[r_Y])
    k.pop()
    k.pop()

    k.pop()

    k.push()
    msf = k.sb("msf", [128, 512], F32); r_msf = R()
    msel = k.sb("msel", [128, 16, 512], BF16); r_msel = R()
    for i in range(16):
        k.dma("sp", lambda e: e.dma_start(out=msf[:], in_=msel_d[i, :, :]), writes=[r_msf])
        k.op("dve", lambda e: e.tensor_copy(out=msel[:, i, :], in_=msf[:]), reads=[r_msf], writes=[r_msel])
    btab = k.sb("btab", [128, NTB * NQC], F32); r_btab = R()
    k.dma("sp", lambda e: e.dma_start(out=btab[:], in_=bias0_d[:, :]), writes=[r_btab])
    sm = k.sb("sm", [128, 8], F32); r_sm = R()
    k.op("dve", lambda e: e.tensor_reduce(out=sm[:, 0:1], in_=qg[:], axis=AX.X, op=ALU.max, apply_absolute_value=True), reads=[r_qg], writes=[r_sm])
    k.op("dve", lambda e: e.tensor_reduce(out=sm[:, 1:2], in_=kg[:], axis=AX.X, op=ALU.max, apply_absolute_value=True), reads=[r_kg], writes=[r_sm])
    k.op("dve", lambda e: e.tensor_tensor(out=sm[:, 2:3], in0=sm[:, 0:1], in1=sm[:, 1:2], op=ALU.mult), reads=[r_sm], writes=[r_sm])
    k.op("dve", lambda e: e.tensor_scalar(out=sm[:, 3:4], in0=sm[:, 2:3], scalar1=-8.0, scalar2=None, op0=ALU.mult), reads=[r_sm], writes=[r_sm])
    k.op("dve", lambda e: e.tensor_scalar(out=btab[:], in0=btab[:], scalar1=sm[:, 3:4], scalar2=None, op0=ALU.add), reads=[r_btab, r_sm], writes=[r_btab])
    lt = k.sb("lt", [128, 4, 64], F32); r_lt = R()
    for i, src in enumerate((lq1_d, lk1_d, lq2_d, lk2_d)):
        k.dma("sp", lambda e: e.dma_start(out=lt[:, i, :], in_=src[0:1, :].to_broadcast([128, 64])), writes=[r_lt])
    lj = k.sb("lj", [128, 64], F32); r_lj = R()
    lam_init = 0.8 - 0.6 * math.exp(-0.3 * 0)
    k.op("dve", lambda e: e.scalar_tensor_tensor(out=lj[:], in0=lt[:, 0, :], scalar=1.0, in1=lt[:, 1, :], op0=ALU.mult, op1=ALU.mult, accum_out=sm[:, 4:5]), reads=[r_lt], writes=[r_lj, r_sm])
    k.op("dve", lambda e: e.scalar_tensor_tensor(out=lj[:], in0=lt[:, 2, :], scalar=1.0, in1=lt[:, 3, :], op0=ALU.mult, op1=ALU.mult, accum_out=sm[:, 5:6]), reads=[r_lt, r_lj], writes=[r_lj, r_sm])
    k.op("act", lambda e: e.activation(out=sm[:, 4:6], in_=sm[:, 4:6], func=AF.Exp), reads=[r_sm], writes=[r_sm])
    k.op("dve", lambda e: e.tensor_tensor(out=sm[:, 6:7], in0=sm[:, 5:6], in1=sm[:, 4:5], op=ALU.subtract), reads=[r_sm], writes=[r_sm])
    k.op("dve", lambda e: e.tensor_scalar(out=sm[:, 6:7], in0=sm[:, 6:7], scalar1=-lam_init, scalar2=None, op0=ALU.add), reads=[r_sm], writes=[r_sm])
    slg = k.sb("slg", [128, 128], F32); r_slg = R()
    k.dma("sp", lambda e: e.dma_start(out=slg[:], in_=sublng_d[0:1, :].to_broadcast([128, 128])), writes=[r_slg])
    k.op("dve", lambda e: e.tensor_scalar(out=slg[:], in0=slg[:], scalar1=float(1.0 - lam_init), scalar2=None, op0=ALU.mult), reads=[r_slg], writes=[r_slg])

    kTh = [k.sb("kTh%d" % i, [128, T], BF16) for i in range(2)]; r_kTh = [R(), R()]
    Vh = [k.sb("Vh%d" % i, [128, NTB, 129], BF16) for i in range(2)]; r_Vh = [R(), R()]
    qTh = [k.sb("qTh%d" % i, [128, TO], BF16) for i in range(2)]; r_qTh = [R(), R()]
    for i in range(2):
        k.op("pool", lambda e: e.memset(Vh[i][:, :, 128:129], 1.0), writes=[r_Vh[i]])
    Pt = [k.sb("Pt%d" % i, [128, 512], BF16) for i in range(3)]; r_Pt = [R(), R(), R()]
    pS = [k.ps("pS%d" % i, [128, 512], F32) for i in range(2)]; r_pS = [R(), R()]
    pO = [k.ps("pO%d" % i, [128, 512], F32) for i in range(4)]; r_pO = [R() for _ in range(4)]
    pTy = k.ps("pTy", [128, 4, 128], BF16); r_pTy = R()
    Oe = k.sb("Oe", [128, 2, 4, 129], F32); r_Oe = R()
    rl = k.sb("rl", [128, 2, 4], F32); r_rl = R()
    o1 = k.sb("o1", [128, 4, 128], F32); r_o1 = R()
    o2 = k.sb("o2", [128, 4, 128], F32); r_o2 = R()
    osq = k.sb("osq", [128, 4, 128], F32); r_osq = R()
    oss = k.sb("oss", [128, 4, 4], F32); r_oss = R()
    yb = k.sb("yb", [128, 4, 128], BF16); r_yb = R()
    yTt = k.sb("yTt", [128, 4, 128], BF16); r_yTt = R()
    it = 0
    for h in range(8):
        kT_, rkT = kTh[h % 2], r_kTh[h % 2]
        V_, rV = Vh[h % 2], r_Vh[h % 2]
        qT_, rqT = qTh[h % 2], r_qTh[h % 2]
        k.dma("sp", lambda e: e.dma_start(out=kT_[:], in_=KT_d[:, h, :]), reads=[r_KT], writes=[rkT])
        k.dma("sp", lambda e: e.dma_start(out=qT_[:], in_=QT_d[:, h, :]), reads=[r_QT], writes=[rqT])
        for n0 in range(0, NTB, 8):
            k.dma("sp", lambda e: e.dma_start(out=V_[:, n0:n0 + 8, 0:128], in_=V_d[n0 * 128:(n0 + 8) * 128, h * 128:(h + 1) * 128].rearrange("(n p) e -> p n e", p=128)),
                  reads=[r_V], writes=[rV])
        for qc in range(NQC):
            for m in range(2):
                for kb in range(NTB):
                    pS_, rpS = pS[it % 2], r_pS[it % 2]
                    P_, rP = Pt[it % 3], r_Pt[it % 3]
                    it += 1
                    k.op("pe", lambda e: e.matmul(pS_[:], lhsT=kT_[m * 64:(m + 1) * 64, kb * 128:(kb + 1) * 128], rhs=qT_[m * 64:(m + 1) * 64, qc * 512:(qc + 1) * 512],
                                                  start=True, stop=True), reads=[rkT, rqT], writes=[rpS])
                    bi = kb * NQC + qc
                    k.op("act", lambda e: e.activation(out=P_[:], in_=pS_[:], func=AF.Exp, bias=btab[:, bi:bi + 1], scale=0.125), reads=[rpS, r_btab], writes=[rP])
                    dlt = kb - 4 * qc
                    if dlt >= 0 and (dlt % NTO) < 4:
                        mi = (dlt // NTO) * 4 + (dlt % NTO)
                        k.op("dve", lambda e: e.tensor_tensor(out=P_[:], in0=P_[:], in1=msel[:, mi, :], op=ALU.mult), reads=[rP, r_msel], writes=[rP])
                    for qs in range(4):
                        k.op("pe", lambda e: e.matmul(pO[qs][:, 0:129], lhsT=P_[:, qs * 128:(qs + 1) * 128], rhs=V_[:, kb, :],
                                                      start=(kb == 0), stop=(kb == NTB - 1)), reads=[rP, rV], writes=[r_pO[qs]])
                for qs in range(4):
                    eng = "act" if qs % 2 == 0 else "dve"
                    if eng == "act":
                        k.op("act", lambda e: e.activation(out=Oe[:, m, qs, :], in_=pO[qs][:, 0:129], func=AF.Copy), reads=[r_pO[qs]], writes=[r_Oe])
                    else:
                        k.op("dve", lambda e: e.tensor_copy(out=Oe[:, m, qs, :], in_=pO[qs][:, 0:129]), reads=[r_pO[qs]], writes=[r_Oe])
            k.op("dve", lambda e: e.reciprocal(out=rl[:], in_=Oe[:, :, :, 128]), reads=[r_Oe], writes=[r_rl])
            k.op("dve", lambda e: e.tensor_scalar(out=rl[:, 1, :], in0=rl[:, 1, :], scalar1=sm[:, 6:7], scalar2=None, op0=ALU.mult), reads=[r_rl, r_sm], writes=[r_rl])
            k.op("dve", lambda e: e.tensor_tensor(out=o1[:], in0=Oe[:, 0, :, 0:128], in1=rl[:, 0, :].unsqueeze(2).to_broadcast([128, 4, 128]), op=ALU.mult), reads=[r_Oe, r_rl], writes=[r_o1])
            k.op("pool", lambda e: e.tensor_tensor(out=o2[:], in0=Oe[:, 1, :, 0:128], in1=rl[:, 1, :].unsqueeze(2).to_broadcast([128, 4, 128]), op=ALU.mult), reads=[r_Oe, r_rl], writes=[r_o2])
            k.op("dve", lambda e: e.tensor_tensor(out=o1[:], in0=o1[:], in1=o2[:], op=ALU.add), reads=[r_o1, r_o2], writes=[r_o1])
            k.op("act", lambda e: e.activation(out=osq[:], in_=o1[:], func=AF.Square), reads=[r_o1], writes=[r_osq])
            k.op("dve", lambda e: e.tensor_reduce(out=oss[:, 0, :], in_=osq[:], axis=AX.X, op=ALU.add), reads=[r_osq], writes=[r_oss])
            k.op("dve", lambda e: e.tensor_scalar(out=oss[:, 1, :], in0=oss[:, 0, :], scalar1=1.0 / 128, scalar2=1e-5, op0=ALU.mult, op1=ALU.add), reads=[r_oss], writes=[r_oss])
            k.op("act", lambda e: e.activation(out=oss[:, 2, :], in_=oss[:, 1, :], func=AF.Sqrt), reads=[r_oss], writes=[r_oss])
            k.op("dve", lambda e: e.reciprocal(out=oss[:, 3, :], in_=oss[:, 2, :]), reads=[r_oss], writes=[r_oss])
            k.op("dve", lambda e: e.tensor_tensor(out=o1[:], in0=o1[:], in1=oss[:, 3, :].unsqueeze(2).to_broadcast([128, 4, 128]), op=ALU.mult), reads=[r_o1, r_oss], writes=[r_o1])
            k.op("dve", lambda e: e.tensor_tensor(out=yb[:], in0=o1[:], in1=slg[:].unsqueeze(1).to_broadcast([128, 4, 128]), op=ALU.mult), reads=[r_o1, r_slg], writes=[r_yb])
            for qs in range(4):
                k.op("pe", lambda e: e.transpose(out=pTy[:, qs, :], in_=yb[:, qs, :], identity=ident[:]), reads=[r_yb, r_ident], writes=[r_pTy])
            k.op("dve", lambda e: e.tensor_copy(out=yTt[:], in_=pTy[:]), reads=[r_pTy], writes=[r_yTt])
            k.dma("sp", lambda e: e.dma_start(out=Y_d[:, 8 + h, qc * 512:(qc + 1) * 512], in_=yTt[:].rearrange("p a b -> p (a b)")), reads=[r_yTt], writes=[r_Y])
    k.pop()

    k.push()
    Wo = k.sb("Wo", [128, DC, 2048], BF16); r_Wo = R()
    wstg = [k.sb("wstgo%d" % i, [128, 2048], F32) for i in range(2)]; r_wstg = [R(), R()]
    load_weight_bf16(Wo, r_Wo, w_out_d, 0, 2048, wstg, r_wstg)
    GA1 = k.sb("GA1", [128, D], F32); r_GA1 = R()
    k.dma("sp", lambda e: e.dma_start(out=GA1[:], in_=MOD_d[0:1, :].to_broadcast([128, D])), reads=[r_MOD], writes=[r_GA1])
    yT = [k.sb("yT%d" % i, [128, DC, 128], BF16) for i in range(2)]; r_yT = [R(), R()]
    xt = [k.sb("xt3_%d" % i, [128, D], F32) for i in range(2)]; r_xt = [R(), R()]
    tmp = k.sb("tmp3", [128, D], F32); r_tmp = R()
    x1 = [k.sb("x1_%d" % i, [128, D], F32) for i in range(2)]; r_x1 = [R(), R()]
    pA = k.ps("pA3", [128, 2048], F32); r_pA = R()
    for ti in range(NTO):
        y_, ry = yT[ti % 2], r_yT[ti % 2]
        x_, rx = xt[ti % 2], r_xt[ti % 2]
        x1_, rx1 = x1[ti % 2], r_x1[ti % 2]
        k.dma("sp", lambda e: e.dma_start(out=y_[:], in_=Y_d[:, :, ti * 128:(ti + 1) * 128]), reads=[r_Y], writes=[ry])
        k.dma("sp", lambda e: e.dma_start(out=x_[:], in_=xo_d[128 + ti * 128:128 + (ti + 1) * 128, :]), writes=[rx])
        for nb in range(4):
            for dc in range(DC):
                k.op("pe", lambda e: e.matmul(pA[:, nb * 512:(nb + 1) * 512], lhsT=y_[:, dc, :], rhs=Wo[:, dc, nb * 512:(nb + 1) * 512],
                                              start=(dc == 0), stop=(dc == DC - 1)), reads=[ry, r_Wo], writes=[r_pA])
        k.op("dve", lambda e: e.tensor_tensor(out=tmp[:], in0=pA[:], in1=GA1[:], op=ALU.mult), reads=[r_pA, r_GA1], writes=[r_tmp])
        k.op("pool", lambda e: e.tensor_tensor(out=x1_[:], in0=tmp[:], in1=x_[:], op=ALU.add), reads=[r_tmp, rx], writes=[rx1])
        k.dma("sp", lambda e: e.dma_start(out=X1_d[ti * 128:(ti + 1) * 128, :], in_=x1_[:]), reads=[rx1], writes=[r_X1])
    k.pop()

    k.push()
    IDX = k.sb("IDX", [128, NTO, 128], I32); r_IDX = R()
    GG = k.sb("GG", [128, NTO, 128], F32); r_GG = R()
    k.push()
    Wp = k.sb("Wp", [128, DC, 2048], BF16); r_Wp = R()
    k.push()
    wstg = [k.sb("wstgp%d" % i, [128, 2048], F32) for i in range(2)]; r_wstg = [R(), R()]
    load_weight_bf16(Wp, r_Wp, wpq_d, 0, 2048, wstg, r_wstg)
    k.pop()
    G2 = k.sb("G2", [128, D], F32); r_G2 = R()
    SH2 = k.sb("SH2", [128, D], F32); r_SH2 = R()
    k.dma("sp", lambda e: e.dma_start(out=G2[:], in_=MOD_d[1:2, :].to_broadcast([128, D])), reads=[r_MOD], writes=[r_G2])
    k.dma("sp", lambda e: e.dma_start(out=SH2[:], in_=MOD_d[2:3, :].to_broadcast([128, D])), reads=[r_MOD], writes=[r_SH2])
    pA = k.ps("pA4", [128, 2048], F32); r_pA = R()
    pT = k.ps("pT4", [128, DC, 128], BF16); r_pT = R()
    kyf = k.sb("kyf", [128, 2, 128], F32); r_kyf = R()
    kyb = k.sb("kyb", [128, 2, 128], BF16); r_kyb = R()
    kyT = k.sb("kyT", [128, 2, 128], BF16); r_kyT = R()
    k.dma("sp", lambda e: e.dma_start(out=kyf[:, 0, :], in_=sk1_d[:, :]), writes=[r_kyf])
    k.dma("sp", lambda e: e.dma_start(out=kyf[:, 1, :], in_=sk2_d[:, :]), writes=[r_kyf])
    k.op("dve", lambda e: e.tensor_copy(out=kyb[:], in_=kyf[:]), reads=[r_kyf], writes=[r_kyb])
    transpose_cols(kyb[:].rearrange("p a b -> p (a b)"), r_kyb, 2, pT, r_pT, kyT[:], r_kyT, evac="dve")
    xt = [k.sb("xt4_%d" % i, [128, D], F32) for i in range(1)] * 2; r_xt = [R()] * 2
    junk = k.sb("junk4", [128, D], BF16); r_junk = R()
    ss = k.sb("ss4", [128, 4], F32); r_ss = R()
    tmp = k.sb("tmp4", [128, D], F32); r_tmp = R()
    h2f = k.sb("h2f", [128, D], F32); r_h2f = R()
    hb = k.sb("hb4", [128, D], BF16); r_hb = R()
    hT = k.sb("hT4", [128, DC, 128], BF16); r_hT = R()
    qb = k.sb("qb4", [128, D], BF16); r_qb = R()
    qT = k.sb("qT4", [128, DC, 128], BF16); r_qT = R()
    sc = k.sb("sc4", [128, 16, 128], F32); r_sc = R()
    sc2 = k.sb("sc4b", [128, 16, 128], F32); r_sc2 = R()
    v12 = k.sb("v12", [128, 8, 2, 16], F32); r_v12 = R()
    ix = k.sb("ix", [128, 8, 2, 16], U32); r_ix = R()
    ixf = k.sb("ixf", [128, 8, 2, 16], F32); r_ixf = R()
    cand = k.sb("cand", [128, 8, 16, 16], F32); r_cand = R()
    cand2 = k.sb("cand2", [128, 8, 16, 16], F32); r_cand2 = R()
    cidx = k.sb("cidx", [128, 8, 16, 16], F32); r_cidx = R()
    ts = k.sb("ts", [128, 8, 16], F32); r_ts = R()
    eq = k.sb("eq", [128, 16, 256], F32); r_eq = R()
    idxf = k.sb("idxf", [128, 8, 16], F32); r_idxf = R()
    es = k.sb("es", [128, 8, 16], F32); r_es = R()
    esum = k.sb("esum", [128, 2, 8], F32); r_esum = R()
    for ti in range(NTO):
        x_, rx = xt[ti % 2], r_xt[ti % 2]
        k.dma("sp", lambda e: e.dma_start(out=x_[:], in_=X1_d[ti * 128:(ti + 1) * 128, :]), reads=[r_X1], writes=[rx])
        norm_mod(x_, rx, G2, r_G2, SH2, r_SH2, junk, r_junk, ss, r_ss, tmp, r_tmp, hb, r_hb, hf=h2f, r_hf=r_h2f)
        k.dma("sp", lambda e: e.dma_start(out=H2_d[ti * 128:(ti + 1) * 128, :], in_=h2f[:]), reads=[r_h2f], writes=[r_H2])
        transpose_cols(hb, r_hb, DC, pT, r_pT, hT[:], r_hT)
        for nb in range(4):
            for dc in range(DC):
                k.op("pe", lambda e: e.matmul(pA[:, nb * 512:(nb + 1) * 512], lhsT=hT[:, dc, :], rhs=Wp[:, dc, nb * 512:(nb + 1) * 512],
                                              start=(dc == 0), stop=(dc == DC - 1)), reads=[r_hT, r_Wp], writes=[r_pA])
        k.op("act", lambda e: e.activation(out=qb[:], in_=pA[:], func=AF.Copy), reads=[r_pA], writes=[r_qb])
        transpose_cols(qb, r_qb, DC, pT, r_pT, qT[:], r_qT, evac="dve")
        for g in range(16):
            k.op("pe", lambda e: e.matmul(pA[:, g * 128:(g + 1) * 128], lhsT=qT[:, g, :], rhs=kyT[:, g % 2, :], start=True, stop=True),
                 reads=[r_qT, r_kyT], writes=[r_pA])
        k.op("act", lambda e: e.activation(out=sc[:].rearrange("p g n -> p (g n)"), in_=pA[:], func=AF.Copy), reads=[r_pA], writes=[r_sc])
        for g in range(16):
            hh, hf_ = g // 2, g % 2
            k.op("dve", lambda e: e.max(out=v12[:, hh, hf_, 0:8], in_=sc[:, g, :]), reads=[r_sc], writes=[r_v12])
            k.op("dve", lambda e: e.match_replace(out=sc2[:, g, :], in_to_replace=v12[:, hh, hf_, 0:8], in_values=sc[:, g, :], imm_value=-1e30), reads=[r_sc, r_v12], writes=[r_sc2])
            k.op("dve", lambda e: e.max(out=v12[:, hh, hf_, 8:16], in_=sc2[:, g, :]), reads=[r_sc2], writes=[r_v12])
            k.op("dve", lambda e: e.max_index(out=ix[:, hh, hf_, 0:8], in_max=v12[:, hh, hf_, 0:8], in_values=sc[:, g, :]), reads=[r_sc, r_v12], writes=[r_ix])
            k.op("dve", lambda e: e.max_index(out=ix[:, hh, hf_, 8:16], in_max=v12[:, hh, hf_, 8:16], in_values=sc2[:, g, :]), reads=[r_sc2, r_v12], writes=[r_ix])
        k.op("dve", lambda e: e.tensor_copy(out=ixf[:], in_=ix[:]), reads=[r_ix], writes=[r_ixf])
        k.op("dve", lambda e: e.tensor_scalar(out=ixf[:, :, 0, :], in0=ixf[:, :, 0, :], scalar1=128.0, scalar2=None, op0=ALU.mult), reads=[r_ixf], writes=[r_ixf])
        k.op("dve", lambda e: e.tensor_tensor(out=cand[:], in0=v12[:, :, 0, :].unsqueeze(3).to_broadcast([128, 8, 16, 16]),
                                              in1=v12[:, :, 1, :].unsqueeze(2).to_broadcast([128, 8, 16, 16]), op=ALU.add), reads=[r_v12], writes=[r_cand])
        k.op("pool", lambda e: e.tensor_tensor(out=cidx[:], in0=ixf[:, :, 0, :].unsqueeze(3).to_broadcast([128, 8, 16, 16]),
                                               in1=ixf[:, :, 1, :].unsqueeze(2).to_broadcast([128, 8, 16, 16]), op=ALU.add), reads=[r_ixf], writes=[r_cidx])
        for hh in range(8):
            cv = cand[:, hh, :, :].rearrange("p a b -> p (a b)")
            cv2 = cand2[:, hh, :, :].rearrange("p a b -> p (a b)")
            k.op("dve", lambda e: e.max(out=ts[:, hh, 0:8], in_=cv), reads=[r_cand], writes=[r_ts])
            k.op("dve", lambda e: e.match_replace(out=cv2, in_to_replace=ts[:, hh, 0:8], in_values=cv, imm_value=-1e30), reads=[r_cand, r_ts], writes=[r_cand2])
            k.op("dve", lambda e: e.max(out=ts[:, hh, 8:16], in_=cv2), reads=[r_cand2], writes=[r_ts])
        for hh in range(8):
            cv = cand[:, hh, :, :].rearrange("p a b -> p (a b)")
            ci = cidx[:, hh, :, :].rearrange("p a b -> p (a b)")
            k.op("dve", lambda e: e.tensor_tensor(out=eq[:], in0=cv.unsqueeze(1).to_broadcast([128, 16, 256]),
                                                  in1=ts[:, hh, :].unsqueeze(2).to_broadcast([128, 16, 256]), op=ALU.is_equal), reads=[r_cand, r_ts], writes=[r_eq])
            k.op("dve", lambda e: e.tensor_tensor(out=eq[:], in0=eq[:], in1=ci.unsqueeze(1).to_broadcast([128, 16, 256]), op=ALU.mult), reads=[r_eq, r_cidx], writes=[r_eq])
            k.op("dve", lambda e: e.tensor_reduce(out=idxf[:, hh, :], in_=eq[:], axis=AX.X, op=ALU.add), reads=[r_eq], writes=[r_idxf])
        k.op("dve", lambda e: e.tensor_scalar(out=idxf[:], in0=idxf[:], scalar1=16383.0, scalar2=0.0, op0=ALU.min, op1=ALU.max), reads=[r_idxf], writes=[r_idxf])
        k.op("dve", lambda e: e.tensor_copy(out=IDX[:, ti, :], in_=idxf[:].rearrange("p h k -> p (h k)")), reads=[r_idxf], writes=[r_IDX])
        k.op("dve", lambda e: e.tensor_tensor(out=es[:], in0=ts[:], in1=ts[:, :, 0:1].to_broadcast([128, 8, 16]), op=ALU.subtract), reads=[r_ts], writes=[r_es])
        k.op("act", lambda e: e.activation(out=es[:], in_=es[:], func=AF.Exp), reads=[r_es], writes=[r_es])
        k.op("dve", lambda e: e.tensor_reduce(out=esum[:, 0, :], in_=es[:], axis=AX.X, op=ALU.add), reads=[r_es], writes=[r_esum])
        k.op("dve", lambda e: e.reciprocal(out=esum[:, 1, :], in_=esum[:, 0, :]), reads=[r_esum], writes=[r_esum])
        k.op("dve", lambda e: e.tensor_tensor(out=GG[:, ti, :].rearrange("p (h k) -> p h k", k=16), in0=es[:], in1=esum[:, 1, :].unsqueeze(2).to_broadcast([128, 8, 16]), op=ALU.mult),
             reads=[r_es, r_esum], writes=[r_GG])
    k.pop()

    k.push()
    GA2 = k.sb("GA2", [128, D], F32); r_GA2 = R()
    k.dma("sp", lambda e: e.dma_start(out=GA2[:], in_=MOD_d[3:4, :].to_broadcast([128, D])), reads=[r_MOD], writes=[r_GA2])
    NB = 4
    ub = [k.sb("ub%d" % i, [128, D], F32) for i in range(NB)]; r_ub = [R() for _ in range(NB)]
    vb = [k.sb("vb%d" % i, [128, D], F32) for i in range(NB)]; r_vb = [R() for _ in range(NB)]
    h2 = k.sb("h2t", [128, D], F32); r_h2 = R()
    x1t = k.sb("x1t", [128, D], F32); r_x1t = R()
    junkf = k.sb("junkf", [128, D], F32); r_junkf = R()
    av = k.sb("av", [128, 128], F32); r_av = R()
    wv = k.sb("wv", [128, 128], F32); r_wv = R()
    facc = k.sb("facc", [128, D], F32); r_facc = R()
    ot = k.sb("ot", [128, D], F32); r_ot = R()
    for ti in range(NTO):
        k.dma("sp", lambda e: e.dma_start(out=h2[:], in_=H2_d[ti * 128:(ti + 1) * 128, :]), reads=[r_H2], writes=[r_h2])
        k.dma("sp", lambda e: e.dma_start(out=x1t[:], in_=X1_d[ti * 128:(ti + 1) * 128, :]), reads=[r_X1], writes=[r_x1t])
        for j in range(128):
            u_, ru = ub[j % NB], r_ub[j % NB]
            k.dma("pool", lambda e: e.indirect_dma_start(out=u_[:], out_offset=None, in_=eu_d[:, :],
                                                         in_offset=bass.IndirectOffsetOnAxis(ap=IDX[:, ti, j:j + 1], axis=0)), reads=[r_IDX], writes=[ru])
            k.op("dve", lambda e: e.scalar_tensor_tensor(out=junkf[:], in0=u_[:], scalar=1.0, in1=h2[:], op0=ALU.mult, op1=ALU.mult, accum_out=av[:, j:j + 1]),
                 reads=[ru, r_h2], writes=[r_junkf, r_av])
        k.op("act", lambda e: e.activation(out=wv[:], in_=av[:], func=AF.Gelu), reads=[r_av], writes=[r_wv])
        k.op("dve", lambda e: e.tensor_tensor(out=wv[:], in0=wv[:], in1=GG[:, ti, :], op=ALU.mult), reads=[r_wv, r_GG], writes=[r_wv])
        for j in range(128):
            v_, rv = vb[j % NB], r_vb[j % NB]
            k.dma("pool", lambda e: e.indirect_dma_start(out=v_[:], out_offset=None, in_=ev_d[:, :],
                                                         in_offset=bass.IndirectOffsetOnAxis(ap=IDX[:, ti, j:j + 1], axis=0)), reads=[r_IDX], writes=[rv])
            if j == 0:
                k.op("dve", lambda e: e.tensor_scalar(out=facc[:], in0=v_[:], scalar1=wv[:, 0:1], scalar2=None, op0=ALU.mult), reads=[rv, r_wv], writes=[r_facc])
            else:
                k.op("dve", lambda e: e.scalar_tensor_tensor(out=facc[:], in0=v_[:], scalar=wv[:, j:j + 1], in1=facc[:], op0=ALU.mult, op1=ALU.add),
                     reads=[rv, r_wv, r_facc], writes=[r_facc])
        k.op("dve", lambda e: e.tensor_tensor(out=ot[:], in0=facc[:], in1=GA2[:], op=ALU.mult), reads=[r_facc, r_GA2], writes=[r_ot])
        k.op("dve", lambda e: e.tensor_tensor(out=ot[:], in0=ot[:], in1=x1t[:], op=ALU.add), reads=[r_ot, r_x1t], writes=[r_ot])
        k.dma("sp", lambda e: e.dma_start(out=out_d[ti * 128:(ti + 1) * 128, :], in_=ot[:]), reads=[r_ot], writes=[r_out])
    k.pop()
    k.pop()
    k.barrier()
    return nc, k


def make_in_maps(S, inputs):
    T = S
    TO = T // 4
    NTB = T // 128
    NTO = TO // 128
    NQC = TO // 512
    x = np.ascontiguousarray(inputs["x"], dtype=np.float32)
    pos = np.ascontiguousarray(inputs["positions"], dtype=np.int32)
    c = np.ascontiguousarray(inputs["c"], dtype=np.float32)
    shared = {}
    for name in ("w_ada", "b_ada", "norm1_g", "w_in", "conv_w", "q_norm_g", "k_norm_g", "lambda_q1", "lambda_k1",
                 "lambda_q2", "lambda_k2", "subln_g", "w_out", "norm2_g", "w_peer_q", "sub_keys1", "sub_keys2",
                 "expert_u", "expert_v"):
        a = np.asarray(inputs[name], dtype=np.float32)
        shared[name] = np.ascontiguousarray(a[0])
    kk = np.arange(128)[:, None]
    qq = np.arange(512)[None, :]
    in_maps = []
    for core in range(8):
        b, r = core // 4, core % 4
        m = dict(shared)
        m["xb"] = x[b]
        xo = np.zeros((128 + TO, D), np.float32)
        xo[128:] = x[b, r * TO:(r + 1) * TO]
        if r > 0:
            xo[:128] = x[b, r * TO - 128:r * TO]
        m["xo"] = xo
        m["c"] = c[b:b + 1]
        m["posb"] = pos[b].reshape(NTB, 128)
        m["poso"] = pos[b, r * TO:(r + 1) * TO].reshape(NTO, 128)
        m["halo"] = np.full((128, 1), 1.0 if r > 0 else 0.0, np.float32)
        b0 = np.zeros((128, NTB * NQC), np.float32)
        for kb in range(NTB):
            for qc in range(NQC):
                delta = kb * 128 - qc * 512 - r * TO
                if delta >= 512:
                    b0[:, kb * NQC + qc] = NEG
        m["bias0"] = b0
        ms = np.zeros((16, 128, 512), np.float32)
        for rp in range(4):
            for j in range(4):
                if rp < r:
                    ms[rp * 4 + j] = 1.0
                elif rp == r:
                    ms[rp * 4 + j] = ((j * 128 + kk) <= qq).astype(np.float32)
        m["msel"] = ms
        in_maps.append(m)
    return in_maps


def kernel(**inputs):
    S = inputs["x"].shape[1]
    nc, _ = build(S)
    in_maps = make_in_maps(S, inputs)
    res = run_bass_kernel_spmd(nc, in_maps, core_ids=list(range(8)))
    B = inputs["x"].shape[0]
    TO = S // 4
    out = np.zeros((B, S, D), np.float32)
    for core in range(8):
        b, r = core // 4, core % 4
        out[b, r * TO:(r + 1) * TO] = res.results[core]["out"]
    return out
```
